# Optimizing a Trainium2 kernel written in Bass

```python
import math
import jax, jax.numpy as jnp
from jax import lax
import numpy as np

D_MODEL = 1024
BATCH = 4
SEQ = 4096
DEPTH = 2

N_EVEN = (DEPTH + 1) // 2
N_ODD = DEPTH // 2
ALPHA = (2.0 * DEPTH) ** 0.25
BETA = (8.0 * DEPTH) ** -0.25
LN_EPS = 1e-5

CONV_DIM = D_MODEL // 2
CONV_WIDTH = 31
SGU_DIM = D_MODEL // 2
SGU_GROUPS = 4
SGU_GROUP_DIM = SGU_DIM // SGU_GROUPS
CHUNK = 128
AB_IN = 2 * CONV_DIM + 2 * SGU_DIM
AB_MIX = CONV_DIM + SGU_DIM

RWKV_HEAD = 64
RWKV_HEADS = D_MODEL // RWKV_HEAD
DECAY_LORA = 64
AAA_LORA = 64
GATE_LORA = 128
GN_EPS = 64e-5

FFN_HIDDEN = -(-8 * D_MODEL // (3 * 256)) * 256

kernel_name = 'hybrid_conv_sgu_rwkv7_deepnorm'


def _layer_norm(x, g, b, eps=LN_EPS):
    xf = x.astype(jnp.float32)
    mu = jnp.mean(xf, axis=-1, keepdims=True)
    var = jnp.mean(jnp.square(xf - mu), axis=-1, keepdims=True)
    return ((xf - mu) * lax.rsqrt(var + eps) * g + b).astype(x.dtype)


def conv_sgu_mixer(x, w_in, b_in, conv_w, conv_b, cn_g, cn_b, sn_g, sn_b, sgu_w, sgu_b, w_out, b_out):
    bsz, s, _ = x.shape
    h = jnp.einsum('bsd,de->bse', x, w_in) + b_in
    a_val, a_gate, u, v = jnp.split(h, [CONV_DIM, 2 * CONV_DIM, 2 * CONV_DIM + SGU_DIM], axis=-1)
    y = a_val * jax.nn.sigmoid(a_gate)
    y = lax.conv_general_dilated(
        y, conv_w, window_strides=(1,), padding=((CONV_WIDTH - 1, 0),),
        dimension_numbers=('NWC', 'WIO', 'NWC'), feature_group_count=CONV_DIM) + conv_b
    y = jax.nn.silu(_layer_norm(y, cn_g, cn_b))
    u = jax.nn.gelu(u)
    v = _layer_norm(jax.nn.gelu(v), sn_g, sn_b)
    v = v.reshape(bsz, s // CHUNK, CHUNK, SGU_GROUPS, SGU_GROUP_DIM)
    causal = jnp.tril(jnp.ones((CHUNK, CHUNK), dtype=bool))
    w_s = jnp.where(causal[None], sgu_w, jnp.zeros((), sgu_w.dtype))
    sv = jnp.einsum('gts,bcsgd->bctgd', w_s, v) + jnp.transpose(sgu_b)[:, :, None]
    z = u * sv.reshape(bsz, s, SGU_DIM)
    mixed = jnp.concatenate([y, z], axis=-1)
    return jnp.einsum('bse,ed->bsd', mixed, w_out) + b_out


def rwkv7_time_mix(x, mu, w_rkv, w0, w_w1, w_w2, a0, a_w1, a_w2, g_w1, g_w2,
                   k_k, k_a, r_k, ln_g, ln_b, w_out):
    bsz, s, d = x.shape
    x_prev = jnp.pad(x, ((0, 0), (1, 0), (0, 0)))[:, :-1]
    xx = x_prev - x
    xr, xw, xk, xv, xa, xg = [x + xx * mu[i] for i in range(6)]
    r, k, v = jnp.einsum('nbsd,nde->nbse', jnp.stack([xr, xk, xv]), w_rkv)
    w_log = -jax.nn.softplus(-(w0 + jnp.tanh(xw @ w_w1) @ w_w2)) - 0.5
    decay = jnp.exp(-jnp.exp(w_log.astype(jnp.float32)))
    a = jax.nn.sigmoid(a0 + (xa @ a_w1) @ a_w2)
    g = jax.nn.sigmoid(xg @ g_w1) @ g_w2
    hs = (bsz, s, RWKV_HEADS, RWKV_HEAD)
    kk = (k * k_k).reshape(hs).astype(jnp.float32)
    kk = kk / jnp.maximum(jnp.linalg.norm(kk, axis=-1, keepdims=True), 1e-12)
    k = k * (1 + (a - 1) * k_a)
    rf = r.reshape(hs).astype(jnp.float32)
    kf = k.reshape(hs).astype(jnp.float32)
    vf = v.reshape(hs).astype(jnp.float32)
    af = a.reshape(hs).astype(jnp.float32)
    wf = decay.reshape(hs)

    def step(state, inp):
        r_t, w_t, k_t, v_t, kk_t, a_t = inp
        sa = jnp.einsum('bhij,bhj->bhi', state, -kk_t)
        state = (state * w_t[:, :, None, :]
                 + sa[..., None] * (kk_t * a_t)[:, :, None, :]
                 + v_t[..., None] * k_t[:, :, None, :])
        return state, jnp.einsum('bhij,bhj->bhi', state, r_t)

    seq_first = lambda t: jnp.swapaxes(t, 0, 1)
    state0 = jnp.zeros((bsz, RWKV_HEADS, RWKV_HEAD, RWKV_HEAD), jnp.float32)
    _, y = lax.scan(step, state0, tuple(seq_first(t) for t in (rf, wf, kf, vf, kk, af)))
    y = jnp.swapaxes(y, 0, 1)
    ym = jnp.mean(y, axis=-1, keepdims=True)
    yv = jnp.mean(jnp.square(y - ym), axis=-1, keepdims=True)
    y = ((y - ym) * lax.rsqrt(yv + GN_EPS)).reshape(bsz, s, d) * ln_g + ln_b
    bonus = jnp.sum(rf * kf * r_k, axis=-1, keepdims=True) * vf
    y = y + bonus.reshape(bsz, s, d)
    return ((y * g) @ w_out).astype(x.dtype)


def swiglu_ffn(x, w_in, w_out):
    gate, up = jnp.split(x @ w_in, 2, axis=-1)
    return (jax.nn.silu(gate) * up) @ w_out


def setup_inputs(seed: int = 0) -> dict:
    key = jax.random.key(seed)
    ks = iter(jax.random.split(key, 48))
    nrm = lambda shape, scale: scale * jax.random.normal(next(ks), shape, jnp.float32)
    gain = lambda shape: 1.0 + nrm(shape, 0.02)
    D = D_MODEL
    inp = {}
    inp['x'] = nrm((BATCH, SEQ, D), 1.0)
    inp['ab_w_in'] = nrm((N_EVEN, D, AB_IN), D ** -0.5)
    inp['ab_b_in'] = nrm((N_EVEN, AB_IN), 0.02)
    inp['conv_w'] = nrm((N_EVEN, CONV_WIDTH, 1, CONV_DIM), CONV_WIDTH ** -0.5)
    inp['conv_b'] = nrm((N_EVEN, CONV_DIM), 0.02)
    inp['conv_norm_g'] = gain((N_EVEN, CONV_DIM))
    inp['conv_norm_b'] = nrm((N_EVEN, CONV_DIM), 0.02)
    inp['sgu_norm_g'] = gain((N_EVEN, SGU_DIM))
    inp['sgu_norm_b'] = nrm((N_EVEN, SGU_DIM), 0.02)
    inp['sgu_w'] = nrm((N_EVEN, SGU_GROUPS, CHUNK, CHUNK), CHUNK ** -0.5)
    inp['sgu_b'] = 1.0 + nrm((N_EVEN, SGU_GROUPS, CHUNK), 0.1)
    inp['ab_w_out'] = nrm((N_EVEN, AB_MIX, D), BETA * AB_MIX ** -0.5)
    inp['ab_b_out'] = nrm((N_EVEN, D), 0.02)
    inp['rwkv_mu'] = jax.random.uniform(next(ks), (N_ODD, 6, D), jnp.float32)
    inp['rwkv_w_rkv'] = nrm((N_ODD, 3, D, D), D ** -0.5)
    inp['rwkv_w0'] = jax.random.uniform(next(ks), (N_ODD, D), jnp.float32, -6.0, 0.0)
    inp['rwkv_w_w1'] = nrm((N_ODD, D, DECAY_LORA), D ** -0.5)
    inp['rwkv_w_w2'] = nrm((N_ODD, DECAY_LORA, D), 0.1 * DECAY_LORA ** -0.5)
    inp['rwkv_a0'] = nrm((N_ODD, D), 0.1)
    inp['rwkv_a_w1'] = nrm((N_ODD, D, AAA_LORA), D ** -0.5)
    inp['rwkv_a_w2'] = nrm((N_ODD, AAA_LORA, D), 0.1 * AAA_LORA ** -0.5)
    inp['rwkv_g_w1'] = nrm((N_ODD, D, GATE_LORA), D ** -0.5)
    inp['rwkv_g_w2'] = nrm((N_ODD, GATE_LORA, D), GATE_LORA ** -0.5)
    inp['rwkv_k_k'] = 0.85 + nrm((N_ODD, D), 0.02)
    inp['rwkv_k_a'] = gain((N_ODD, D))
    inp['rwkv_r_k'] = nrm((N_ODD, RWKV_HEADS, RWKV_HEAD), 0.1)
    inp['rwkv_ln_g'] = gain((N_ODD, D))
    inp['rwkv_ln_b'] = nrm((N_ODD, D), 0.02)
    inp['rwkv_w_out'] = nrm((N_ODD, D, D), BETA * D ** -0.5)
    inp['ffn_w_in'] = nrm((DEPTH, D, 2 * FFN_HIDDEN), D ** -0.5)
    inp['ffn_w_out'] = nrm((DEPTH, FFN_HIDDEN, D), BETA * FFN_HIDDEN ** -0.5)
    inp['ln_mix_g'] = gain((DEPTH, D))
    inp['ln_mix_b'] = nrm((DEPTH, D), 0.02)
    inp['ln_ffn_g'] = gain((DEPTH, D))
    inp['ln_ffn_b'] = nrm((DEPTH, D), 0.02)
    return inp


def reference(x, ab_w_in, ab_b_in, conv_w, conv_b, conv_norm_g, conv_norm_b,
              sgu_norm_g, sgu_norm_b, sgu_w, sgu_b, ab_w_out, ab_b_out,
              rwkv_mu, rwkv_w_rkv, rwkv_w0, rwkv_w_w1, rwkv_w_w2, rwkv_a0,
              rwkv_a_w1, rwkv_a_w2, rwkv_g_w1, rwkv_g_w2, rwkv_k_k, rwkv_k_a,
              rwkv_r_k, rwkv_ln_g, rwkv_ln_b, rwkv_w_out,
              ffn_w_in, ffn_w_out, ln_mix_g, ln_mix_b, ln_ffn_g, ln_ffn_b):
    for layer in range(DEPTH):
        i = layer // 2
        if layer % 2 == 0:
            mix = conv_sgu_mixer(x, ab_w_in[i], ab_b_in[i], conv_w[i], conv_b[i],
                                 conv_norm_g[i], conv_norm_b[i], sgu_norm_g[i], sgu_norm_b[i],
                                 sgu_w[i], sgu_b[i], ab_w_out[i], ab_b_out[i])
        else:
            mix = rwkv7_time_mix(x, rwkv_mu[i], rwkv_w_rkv[i], rwkv_w0[i], rwkv_w_w1[i],
                                 rwkv_w_w2[i], rwkv_a0[i], rwkv_a_w1[i], rwkv_a_w2[i],
                                 rwkv_g_w1[i], rwkv_g_w2[i], rwkv_k_k[i], rwkv_k_a[i],
                                 rwkv_r_k[i], rwkv_ln_g[i], rwkv_ln_b[i], rwkv_w_out[i])
        x = _layer_norm(ALPHA * x + mix, ln_mix_g[layer], ln_mix_b[layer])
        x = _layer_norm(ALPHA * x + swiglu_ffn(x, ffn_w_in[layer], ffn_w_out[layer]),
                        ln_ffn_g[layer], ln_ffn_b[layer])
    return x
```

```python
import contextlib
import numpy as np
import concourse.bass as bass
import concourse.mybir as mybir

F32 = mybir.dt.float32
BF16 = mybir.dt.bfloat16
AF = mybir.ActivationFunctionType
ALU = mybir.AluOpType
AX = mybir.AxisListType

ENGS = ("pe", "act", "dve", "pool", "sp")


def I(name, *a, **k):
    return lambda e: getattr(e, name)(*a, **k)


class Res:
    __slots__ = ("w", "r", "name")

    def __init__(self, name=""):
        self.w = None
        self.r = []
        self.name = name


class Sched:
    def __init__(self, nc):
        self.nc = nc
        self.es = contextlib.ExitStack()
        self.stream = {e: [] for e in ENGS}
        self.cnt = {e: 0 for e in ENGS}
        self.seen = {e: {} for e in ENGS}
        self.sems = {}
        self.dma_cnt = {}
        for e in ENGS:
            self.sems[e] = self.es.enter_context(nc.semaphore("s_" + e))

    def sbuf(self, name, shape, dtype):
        return self.es.enter_context(self.nc.sbuf_tensor("sb_" + name, list(shape), dtype))

    def psum(self, name, shape, dtype=F32):
        return self.es.enter_context(self.nc.psum_tensor("ps_" + name, list(shape), dtype))

    def dma_sem(self, key):
        if key not in self.sems:
            self.sems[key] = self.es.enter_context(self.nc.semaphore("d_" + str(key)))
            self.dma_cnt[key] = 0
        return key

    def _waits(self, eng, reads, writes):
        need = {}

        def add(ev, same_ok):
            if ev is None:
                return
            k, v = ev
            if k == eng and (eng == 'pe' or eng == 'sp'):
                return
            if v > need.get(k, 0):
                need[k] = v

        for r in reads:
            add(r.w, True)
        for w in writes:
            add(w.w, False)
            for ev in w.r:
                add(ev, False)
        out = []
        seen = self.seen[eng]
        for k, v in need.items():
            if seen.get(k, 0) >= v:
                continue
            seen[k] = v
            out.append((k, v))
        return out

    def op(self, eng, fn, reads=(), writes=()):
        waits = self._waits(eng, reads, writes)
        self.cnt[eng] += 1
        ev = (eng, self.cnt[eng])
        self.stream[eng].append((waits, fn, eng, 1))
        for r in reads:
            r.r.append(ev)
        for w in writes:
            w.w = ev
            w.r = []
        return ev

    def dma(self, q, fn, key, reads=(), writes=()):
        self.dma_sem(key)
        waits = self._waits(q, reads, writes)
        self.dma_cnt[key] += 16
        ev = (key, self.dma_cnt[key])
        self.stream[q].append((waits, fn, key, 16))
        for r in reads:
            r.r.append(ev)
        for w in writes:
            w.w = ev
            w.r = []
        return ev

    def wait_all(self, eng, evs):
        waits = []
        for ev in evs:
            if ev is None:
                continue
            waits.append(ev)
        self.stream[eng].append((waits, None, None, 0))

    def emit(self):
        nc = self.nc
        engobj = {"pe": "tensor", "act": "scalar", "dve": "vector", "pool": "gpsimd", "sp": "sync"}
        with nc.Block() as block:
            for e in ENGS:
                if not self.stream[e]:
                    continue

                def body(eng, e=e):
                    for waits, fn, key, inc in self.stream[e]:
                        for k, v in waits:
                            eng.wait_ge(self.sems[k], v)
                        if fn is None:
                            continue
                        ins = fn(eng)
                        ins.then_inc(self.sems[key], inc)

                getattr(block, engobj[e])(body)

    def close(self):
        self.es.close()


import os
import numpy as np

ALPHA = (2.0 * 2) ** 0.25
LN_EPS = 1e-5
NTOK = 2048
ST = 512
NST = NTOK // ST
HALO = 32
NR = 8
FG = 6
NFC = 22

_cst_off = {}


def _lay(names):
    off = 0
    d = {}
    for n, w in names:
        d[n] = (off, w)
        off += w
    return d, off


CST_A, NCST_A = _lay([
    ("b_in", 12), ("conv_w", 124), ("conv_b", 4), ("cn_g", 4), ("cn_b", 4), ("halo", 1), ("pad", 3),
    ("b_v", 512), ("sn_g", 512), ("sn_b", 512), ("sgu_b", 2048), ("wsT", 512),
    ("b_out", 1024), ("lnm_g", 1024), ("lnm_b", 1024), ("lnf_g", 1024), ("lnf_b", 1024),
])


class Ctx:
    pass


def common_setup(S, C, cst_d, ncst):
    nc = S.nc
    C.cst = S.sbuf("cst", [128, ncst], F32); C.r_cst = Res()
    S.dma("sp", I("dma_start", out=C.cst[:, :], in_=cst_d), "ld_cst", writes=[C.r_cst])
    C.identf = S.sbuf("identf", [128, 128], F32); C.r_identf = Res()
    C.identb = S.sbuf("identb", [128, 128], BF16); C.r_identb = Res()
    S.op("pool", I("memset", C.identf[:, :], 1.0), writes=[C.r_identf])
    S.op("pool", I("affine_select", out=C.identf[:, :], in_=C.identf[:, :], pattern=[[1, 128]],
                                           compare_op=ALU.is_equal, fill=0.0, base=0, channel_multiplier=-1),
         reads=[C.r_identf], writes=[C.r_identf])
    S.op("dve", I("tensor_copy", out=C.identb[:, :], in_=C.identf[:, :]), reads=[C.r_identf], writes=[C.r_identb])
    C.ring = [S.sbuf("ring%d" % i, [128, 2048], BF16) for i in range(NR)]
    C.r_ring = [Res() for _ in range(NR)]
    C.ring_i = 0
    C.pm = S.psum("pm", [128, 6, 512], F32)
    C.r_pm = [Res() for _ in range(3)]
    C.pm_i = 0
    C.pt = S.psum("pt", [128, 2, 1024], BF16)
    C.r_pt = [Res() for _ in range(2)]
    C.pt_i = 0
    C.scr = [S.sbuf("scr%d" % i, [128, 1024], F32) for i in range(4)]
    C.r_scr = [Res() for _ in range(4)]
    C.scr_i = 0
    C.xb16 = [S.sbuf("xb16_%d" % i, [128, 1024], BF16) for i in range(2)]
    C.r_xb16 = [Res() for _ in range(2)]
    C.xb_i = 0
    C.sm = [S.sbuf("sm%d" % i, [128, 32], F32) for i in range(4)]
    C.r_sm = [Res() for _ in range(4)]
    C.sm_i = 0


def ring_load(S, C, src_ap, n):
    i = C.ring_i % NR
    C.ring_i += 1
    t, r = C.ring[i], C.r_ring[i]
    dst = t[:, 0:n]
    if len(src_ap.shape) == 3:
        dst = dst.rearrange("p (c d) -> p c d", c=src_ap.shape[1])
    S.dma("pool", I("dma_start", out=dst, in_=src_ap), "ring%d" % i, writes=[r])
    return t, r


def get_pair(C):
    i = C.pm_i % 3
    C.pm_i += 1
    return C.pm[:, 2 * i:2 * i + 2, :], C.r_pm[i]


def get_pt(C):
    i = C.pt_i % 2
    C.pt_i += 1
    return C.pt[:, i, :], C.r_pt[i]


def get_scr(C):
    i = C.scr_i % 4
    C.scr_i += 1
    return C.scr[i], C.r_scr[i]


def get_sm(C):
    i = C.sm_i % 4
    C.sm_i += 1
    return C.sm[i], C.r_sm[i]


def cc(C, lay, name, a=0, b=None):
    off, w = lay[name]
    if b is None:
        b = w
    return C.cst[:, off + a: off + b]


def make_T(S, C, src, r_src, dstT, r_dstT, col0, eng_cast="act"):
    i = C.xb_i % 2
    C.xb_i += 1
    xb, r_xb = C.xb16[i], C.r_xb16[i]
    if eng_cast == "act":
        S.op("act", I("copy", out=xb[:, :], in_=src), reads=[r_src], writes=[r_xb])
    else:
        S.op(eng_cast, I("tensor_copy", out=xb[:, :], in_=src), reads=[r_src], writes=[r_xb])
    pt, r_pt = get_pt(C)
    for kc in range(8):
        S.op("pe", I("transpose", pt[:, kc * 128:(kc + 1) * 128], xb[:, kc * 128:(kc + 1) * 128], C.identb[:, :]),
             reads=[r_xb, C.r_identb], writes=[r_pt])
    S.op("dve", I("tensor_copy", out=dstT[:, :, col0:col0 + 128], in_=pt.rearrange("p (k c) -> p k c", k=8)),
         reads=[r_pt], writes=[r_dstT])


def ln_rows(S, C, src, r_src, dst, r_dst, g_ap, b_ap, extra_reads=()):
    st, r_st = get_sm(C)
    S.op("dve", I("bn_stats", out=st[:, 0:6], in_=src[:, 0:512]), reads=[r_src], writes=[r_st])
    S.op("dve", I("bn_stats", out=st[:, 6:12], in_=src[:, 512:1024]), reads=[r_src], writes=[r_st])
    S.op("dve", I("bn_aggr", out=st[:, 12:14], in_=st[:, 0:12]), reads=[r_st], writes=[r_st])
    S.op("dve", I("tensor_scalar_add", out=st[:, 14:15], in0=st[:, 13:14], scalar1=LN_EPS), reads=[r_st], writes=[r_st])
    S.op("act", I("sqrt", out=st[:, 15:16], in_=st[:, 14:15]), reads=[r_st], writes=[r_st])
    S.op("dve", I("reciprocal", out=st[:, 16:17], in_=st[:, 15:16]), reads=[r_st], writes=[r_st])
    S.op("dve", I("tensor_scalar", out=st[:, 17:18], in0=st[:, 12:13], scalar1=-1.0, scalar2=st[:, 16:17],
                                          op0=ALU.mult, op1=ALU.mult), reads=[r_st], writes=[r_st])
    n, r_n = get_scr(C)
    S.op("act", I("activation", out=n[:, :], in_=src, func=AF.Identity, bias=st[:, 17:18], scale=st[:, 16:17]),
         reads=[r_src, r_st], writes=[r_n])
    S.op("pool", I("tensor_mul", out=n[:, :], in0=n[:, :], in1=g_ap), reads=[r_n, C.r_cst], writes=[r_n])
    S.op("pool", I("tensor_add", out=dst, in0=n[:, :], in1=b_ap), reads=[r_n, C.r_cst] + list(extra_reads), writes=[r_dst])


def ffn_block(S, C, lay, xT, r_xT, xres, r_xres, w1_d, w2_d, gname, bname):
    passes = [list(range(p, min(p + FG, NFC))) for p in range(0, NFC, FG)]
    for pi, fcs in enumerate(passes):
        act = C.actb[pi % 2]
        r_act = C.r_actb[pi % 2]
        for i, fc in enumerate(fcs):
            wt, r_w = ring_load(S, C, w1_d[fc], 2048)
            P, r_P = get_pair(C)
            for g in range(2):
                for kc in range(8):
                    S.op("pe", I("matmul", P[:, g, :], lhsT=wt[:, (g * 8 + kc) * 128:(g * 8 + kc + 1) * 128],
                                                              rhs=xT[:, kc, HALO:HALO + ST], start=(kc == 0), stop=(kc == 7)),
                         reads=[r_w, r_xT], writes=[r_P])
            sl, r_sl = get_scr(C)
            S.op("act", I("activation", out=sl[:, 0:512], in_=P[:, 0, :], func=AF.Silu), reads=[r_P], writes=[r_sl])
            S.op("dve", I("tensor_tensor", out=act[:, i, :], in0=P[:, 1, :], in1=sl[:, 0:512], op=ALU.mult),
                 reads=[r_P, r_sl], writes=[r_act])
        slots = []
        for j in range(0, len(fcs), 2):
            n2 = min(2, len(fcs) - j)
            wt, r_w = ring_load(S, C, w2_d[fcs[j]:fcs[j] + n2].rearrange("c p d -> p c d"), 1024 * n2)
            slots.append((wt, r_w))
        for tb in range(ST // 128):
            P, r_P = get_pair(C)
            for i, fc in enumerate(fcs):
                wt, r_w = slots[i // 2]
                for h in range(2):
                    S.op("pe", I("matmul", P[:, h, :], lhsT=act[:, i, tb * 128:(tb + 1) * 128],
                                                                   rhs=wt[:, (i % 2) * 1024 + h * 512:(i % 2) * 1024 + (h + 1) * 512],
                                                                   start=(i == 0), stop=(i == len(fcs) - 1)),
                         reads=[r_w, r_act], writes=[r_P])
            Pf = P.rearrange("p a b -> p (a b)")
            if pi == 0:
                S.op("dve", I("scalar_tensor_tensor", out=xres[:, tb, :], in0=xres[:, tb, :], scalar=ALPHA, in1=Pf,
                                                                         op0=ALU.mult, op1=ALU.add),
                     reads=[r_P, r_xres[tb]], writes=[r_xres[tb]])
            else:
                S.op("dve", I("tensor_tensor", out=xres[:, tb, :], in0=xres[:, tb, :], in1=Pf, op=ALU.add),
                     reads=[r_P, r_xres[tb]], writes=[r_xres[tb]])
    for tb in range(ST // 128):
        ln_rows(S, C, xres[:, tb, :], r_xres[tb], xres[:, tb, :], r_xres[tb], cc(C, lay, gname), cc(C, lay, bname))


def outproj_ln(S, C, lay, srcT, r_srcT, wO_d, xres, r_xres, bias_name, gname, bname):
    slots = []
    for j in range(4):
        wt, r_w = ring_load(S, C, wO_d[2 * j:2 * j + 2].rearrange("c p d -> p c d"), 2048)
        slots.append((wt, r_w))
    for tb in range(ST // 128):
        P, r_P = get_pair(C)
        for kc in range(8):
            wt, r_w = slots[kc // 2]
            for h in range(2):
                S.op("pe", I("matmul", P[:, h, :], lhsT=srcT[:, kc, tb * 128:(tb + 1) * 128],
                                                                 rhs=wt[:, (kc % 2) * 1024 + h * 512:(kc % 2) * 1024 + (h + 1) * 512],
                                                                 start=(kc == 0), stop=(kc == 7)),
                     reads=[r_w, r_srcT], writes=[r_P])
        Pf = P.rearrange("p a b -> p (a b)")
        t, r_t = get_scr(C)
        if bias_name is not None:
            S.op("dve", I("tensor_tensor", out=t[:, :], in0=Pf, in1=cc(C, lay, bias_name), op=ALU.add),
                 reads=[r_P, C.r_cst], writes=[r_t])
            S.op("dve", I("scalar_tensor_tensor", out=t[:, :], in0=xres[:, tb, :], scalar=ALPHA, in1=t[:, :],
                                                                op0=ALU.mult, op1=ALU.add),
                 reads=[r_t, r_xres[tb]], writes=[r_t])
        else:
            S.op("dve", I("scalar_tensor_tensor", out=t[:, :], in0=xres[:, tb, :], scalar=ALPHA, in1=Pf,
                                                                     op0=ALU.mult, op1=ALU.add),
                 reads=[r_P, r_xres[tb]], writes=[r_t])
        ln_rows(S, C, t[:, :], r_t, xres[:, tb, :], r_xres[tb], cc(C, lay, gname), cc(C, lay, bname))


def build_A(stop_after=None, stage=99):
    nc = bass.Bass("TRN2", target_bir_lowering=False)
    lay = CST_A
    xs = nc.dram_tensor("xs", [HALO + NTOK, 1024], F32, kind="ExternalInput").ap()
    cst_d = nc.dram_tensor("cst", [128, NCST_A], F32, kind="ExternalInput").ap()
    wA_d = nc.dram_tensor("wA", [12, 128, 1024], F32, kind="ExternalInput").ap()
    wV_d = nc.dram_tensor("wV", [4, 128, 1024], F32, kind="ExternalInput").ap()
    wO_d = nc.dram_tensor("wO", [8, 128, 1024], F32, kind="ExternalInput").ap()
    w1_d = nc.dram_tensor("w1", [NFC, 128, 2048], F32, kind="ExternalInput").ap()
    w2_d = nc.dram_tensor("w2", [NFC, 128, 1024], F32, kind="ExternalInput").ap()
    out_d = nc.dram_tensor("out", [NTOK, 1024], F32, kind="ExternalOutput").ap()
    S = Sched(nc)
    C = Ctx()
    common_setup(S, C, cst_d, NCST_A)
    NB = ST // 128
    xres = S.sbuf("xres", [128, NB, 1024], F32); r_xres = [Res() for _ in range(NB)]
    xT = S.sbuf("xT", [128, 8, HALO + ST], BF16); r_xT = Res()
    xhalo = S.sbuf("xhalo", [HALO, 1024], F32); r_xhalo = Res()
    xhb = S.sbuf("xhb", [HALO, 1024], BF16); r_xhb = Res()
    C.actb = [S.sbuf("actb%d" % i, [128, FG, ST], BF16) for i in range(2)]; C.r_actb = [Res(), Res()]
    ybuf = S.sbuf("ybuf", [128, 4, 30 + ST], BF16); r_y = Res()
    cbuf = S.sbuf("cbuf", [128, 4, ST], F32); r_c = Res()
    ubuf = S.sbuf("ubuf", [128, 4, ST], BF16); r_u = Res()
    vnb = S.sbuf("vnb", [128, NB, 512], BF16); r_vn = [Res() for _ in range(NB)]
    mixed = S.sbuf("mixed", [128, 8, ST], BF16); r_mixed = Res()
    diag = S.sbuf("diag", [128, 4 * 31, 128], BF16); r_diag = Res()
    wsTb = S.sbuf("wsTb", [128, 4, 128], BF16); r_ws = Res()
    ones32 = S.sbuf("ones32", [128, 128], F32); r_ones = Res()
    statsb = S.sbuf("statsb", [128, 3, ST], F32); r_stb = Res()

    S.op("pool", I("memset", ones32[:, :], 1.0 / 512.0), writes=[r_ones])
    for j in range(4):
        for k in range(31):
            S.op("dve", I("tensor_scalar_mul", out=diag[:, j * 31 + k, :], in0=C.identf[:, :],
                                                              scalar1=cc(C, lay, "conv_w", j * 31 + k, j * 31 + k + 1)),
                 reads=[C.r_identf, C.r_cst], writes=[r_diag])
    wsm = cc(C, lay, "wsT").rearrange("p (g t) -> p g t", g=4)
    S.op("pool", I("affine_select", out=wsm, in_=wsm, pattern=[[0, 4], [1, 128]], compare_op=ALU.is_ge, fill=0.0,
                                           base=0, channel_multiplier=-1), reads=[C.r_cst], writes=[C.r_cst])
    S.op("pool", I("tensor_copy", out=wsTb[:, :, :], in_=wsm), reads=[C.r_cst], writes=[r_ws])

    out_evs = []
    for st in range(NST):
        r0 = HALO + st * ST
        S.dma("sp", I("dma_start", out=xres[:, :, :], in_=xs[r0:r0 + ST, :].rearrange("(tb p) d -> p tb d", p=128)),
              "ld_x", writes=r_xres)
        for tb in range(NB):
            make_T(S, C, xres[:, tb, :], r_xres[tb], xT, r_xT, HALO + tb * 128)
        if st == 0:
            S.dma("sp", I("dma_start", out=xhalo[:, :], in_=xs[0:HALO, :]), "ld_xh", writes=[r_xhalo])
            S.op("act", I("copy", out=xhb[:, :], in_=xhalo[:, :]), reads=[r_xhalo], writes=[r_xhb])
            pt, r_pt = get_pt(C)
            for kc in range(8):
                S.op("pe", I("transpose", pt[:, kc * HALO:(kc + 1) * HALO], xhb[:, kc * 128:(kc + 1) * 128], C.identb[0:HALO, 0:HALO]),
                     reads=[r_xhb, C.r_identb], writes=[r_pt])
            S.op("dve", I("tensor_copy", out=xT[:, :, 0:HALO], in_=pt[:, 0:8 * HALO].rearrange("p (k c) -> p k c", k=8)),
                 reads=[r_pt], writes=[r_xT])
        else:
            S.op("dve", I("tensor_copy", out=ybuf[:, :, 0:30], in_=ybuf[:, :, ST:ST + 30]), reads=[r_y], writes=[r_y])
        for j in range(4 if stage >= 2 else 0):
            wg, r_wg = ring_load(S, C, wA_d[4 + j], 1024)
            wv, r_wv = ring_load(S, C, wA_d[j], 1024)
            P, r_P = get_pair(C)
            for h, (wt, r_w) in enumerate(((wg, r_wg), (wv, r_wv))):
                for kc in range(8):
                    S.op("pe", I("matmul", P[:, h, :], lhsT=wt[:, kc * 128:(kc + 1) * 128],
                                                                     rhs=xT[:, kc, HALO:HALO + ST], start=(kc == 0), stop=(kc == 7)),
                         reads=[r_w, r_xT], writes=[r_P])
            sg, r_sg = get_scr(C)
            if os.environ.get('SKIP_SIG') is None:
                S.op("act", I("activation", out=sg[:, 0:ST], in_=P[:, 0, :], func=AF.Sigmoid, bias=cc(C, lay, "b_in", 4 + j, 5 + j)),
                     reads=[r_P, C.r_cst], writes=[r_sg])
            if os.environ.get('SKIP_STT') is None:
                S.op("dve", I("scalar_tensor_tensor", out=ybuf[:, j, 30:30 + ST], in0=P[:, 1, :], scalar=cc(C, lay, "b_in", j, j + 1),
                                                                  in1=sg[:, 0:ST], op0=ALU.add, op1=ALU.mult),
                     reads=[r_P, r_sg, C.r_cst], writes=[r_y])
            if st == 0 and os.environ.get('NOHALO') is None:
                P2, r_P2 = get_pair(C)
                for h, (wt, r_w) in enumerate(((wg, r_wg), (wv, r_wv))):
                    for kc in range(8):
                        S.op("pe", I("matmul", P2[:, h, 0:30], lhsT=wt[:, kc * 128:(kc + 1) * 128],
                                                                         rhs=xT[:, kc, 2:HALO], start=(kc == 0), stop=(kc == 7)),
                             reads=[r_w, r_xT], writes=[r_P2])
                sg2, r_sg2 = get_scr(C)
                S.op("act", I("activation", out=sg2[:, 0:30], in_=P2[:, 0, 0:30], func=AF.Sigmoid, bias=cc(C, lay, "b_in", 4 + j, 5 + j)),
                     reads=[r_P2, C.r_cst], writes=[r_sg2])
                S.op("dve", I("scalar_tensor_tensor", out=sg2[:, 32:62], in0=P2[:, 1, 0:30], scalar=cc(C, lay, "b_in", j, j + 1),
                                                                  in1=sg2[:, 0:30], op0=ALU.add, op1=ALU.mult),
                     reads=[r_P2, r_sg2, C.r_cst], writes=[r_sg2])
                S.op("dve", I("tensor_scalar_mul", out=ybuf[:, j, 0:30], in0=sg2[:, 32:62], scalar1=cc(C, lay, "halo")),
                     reads=[r_sg2, C.r_cst], writes=[r_y])
        for jj in range(2 if stage >= 3 else 0):
            P, r_P = get_pair(C)
            for h in range(2):
                j = 2 * jj + h
                wt, r_w = ring_load(S, C, wA_d[8 + j], 1024)
                for kc in range(8):
                    S.op("pe", I("matmul", P[:, h, :], lhsT=wt[:, kc * 128:(kc + 1) * 128],
                                                                     rhs=xT[:, kc, HALO:HALO + ST], start=(kc == 0), stop=(kc == 7)),
                         reads=[r_w, r_xT], writes=[r_P])
                S.op("act", I("activation", out=ubuf[:, j, :], in_=P[:, h, :], func=AF.Gelu_apprx_tanh,
                                                             bias=cc(C, lay, "b_in", 8 + j, 9 + j)),
                     reads=[r_P, C.r_cst], writes=[r_u])
        Pv = [get_pair(C), get_pair(C)]
        for q in range(4 if stage >= 4 else 0):
            wt, r_w = ring_load(S, C, wV_d[q], 1024)
            for tb in range(NB):
                P, r_P = Pv[tb // 2]
                for k2 in range(2):
                    kc = 2 * q + k2
                    S.op("pe", I("matmul", P[:, tb % 2, :], lhsT=xT[:, kc, HALO + tb * 128:HALO + (tb + 1) * 128],
                                                                                   rhs=wt[:, k2 * 512:(k2 + 1) * 512], start=(kc == 0), stop=(kc == 7)),
                         reads=[r_w, r_xT], writes=[r_P])
        for tb in range(NB if stage >= 4 else 0):
            P, r_P = Pv[tb // 2]
            vb, r_vb = get_scr(C)
            S.op("dve", I("tensor_tensor", out=vb[:, 0:512], in0=P[:, tb % 2, :], in1=cc(C, lay, "b_v"), op=ALU.add),
                 reads=[r_P, C.r_cst], writes=[r_vb])
            S.op("act", I("activation", out=vb[:, 512:1024], in_=vb[:, 0:512], func=AF.Gelu_apprx_tanh), reads=[r_vb], writes=[r_vb])
            st6, r_st = get_sm(C)
            S.op("dve", I("bn_stats", out=st6[:, 0:6], in_=vb[:, 512:1024]), reads=[r_vb], writes=[r_st])
            S.op("dve", I("bn_aggr", out=st6[:, 12:14], in_=st6[:, 0:6]), reads=[r_st], writes=[r_st])
            S.op("dve", I("tensor_scalar_add", out=st6[:, 14:15], in0=st6[:, 13:14], scalar1=LN_EPS), reads=[r_st], writes=[r_st])
            S.op("act", I("sqrt", out=st6[:, 15:16], in_=st6[:, 14:15]), reads=[r_st], writes=[r_st])
            S.op("dve", I("reciprocal", out=st6[:, 16:17], in_=st6[:, 15:16]), reads=[r_st], writes=[r_st])
            S.op("dve", I("tensor_scalar", out=st6[:, 17:18], in0=st6[:, 12:13], scalar1=-1.0, scalar2=st6[:, 16:17],
                                                  op0=ALU.mult, op1=ALU.mult), reads=[r_st], writes=[r_st])
            S.op("act", I("activation", out=vb[:, 0:512], in_=vb[:, 512:1024], func=AF.Identity, bias=st6[:, 17:18], scale=st6[:, 16:17]),
                 reads=[r_vb, r_st], writes=[r_vb])
            S.op("pool", I("tensor_mul", out=vb[:, 0:512], in0=vb[:, 0:512], in1=cc(C, lay, "sn_g")), reads=[r_vb, C.r_cst], writes=[r_vb])
            S.op("pool", I("tensor_add", out=vnb[:, tb, :], in0=vb[:, 0:512], in1=cc(C, lay, "sn_b")),
                 reads=[r_vb, C.r_cst], writes=[r_vn[tb]])
        if stage < 5:
            ev = S.dma("sp", I("dma_start", out=out_d[st * ST:(st + 1) * ST, :].rearrange("(tb p) d -> p tb d", p=128), in_=xres[:, :, :]),
                       "st_out", reads=r_xres)
            out_evs.append(ev)
            continue
        Pst, r_Pst = get_pair(C)
        for jj in range(2):
            P, r_P = get_pair(C)
            for h in range(2):
                j = 2 * jj + h
                for k in range(31):
                    S.op("pe", I("matmul", P[:, h, :], lhsT=diag[:, j * 31 + k, :], rhs=ybuf[:, j, k:k + ST],
                                                                 start=(k == 0), stop=(k == 30)),
                         reads=[r_diag, r_y], writes=[r_P])
                S.op("act", I("activation", out=cbuf[:, j, :], in_=P[:, h, :], func=AF.Identity, bias=cc(C, lay, "conv_b", j, j + 1)),
                     reads=[r_P, C.r_cst], writes=[r_c])
                sq, r_sq = get_scr(C)
                S.op("act", I("activation", out=sq[:, 0:ST], in_=P[:, h, :], func=AF.Square, bias=cc(C, lay, "conv_b", j, j + 1)),
                     reads=[r_P, C.r_cst], writes=[r_sq])
                S.op("pe", I("matmul", Pst[:, 0, :], lhsT=ones32[:, :], rhs=cbuf[:, j, :], start=(j == 0), stop=(j == 3)),
                     reads=[r_ones, r_c], writes=[r_Pst])
                S.op("pe", I("matmul", Pst[:, 1, :], lhsT=ones32[:, :], rhs=sq[:, 0:ST], start=(j == 0), stop=(j == 3)),
                     reads=[r_ones, r_sq], writes=[r_Pst])
        S.op("act", I("copy", out=statsb[:, 0, :], in_=Pst[:, 0, :]), reads=[r_Pst], writes=[r_stb])
        S.op("pool", I("tensor_mul", out=statsb[:, 1, :], in0=statsb[:, 0, :], in1=statsb[:, 0, :]), reads=[r_stb], writes=[r_stb])
        S.op("dve", I("tensor_tensor", out=statsb[:, 1, :], in0=Pst[:, 1, :], in1=statsb[:, 1, :], op=ALU.subtract), reads=[r_Pst, r_stb], writes=[r_stb])
        S.op("dve", I("tensor_scalar_add", out=statsb[:, 1, :], in0=statsb[:, 1, :], scalar1=LN_EPS), reads=[r_stb], writes=[r_stb])
        S.op("act", I("sqrt", out=statsb[:, 1, :], in_=statsb[:, 1, :]), reads=[r_stb], writes=[r_stb])
        S.op("dve", I("reciprocal", out=statsb[:, 2, :], in_=statsb[:, 1, :]), reads=[r_stb], writes=[r_stb])
        for j in range(4):
            t, r_t = get_scr(C)
            S.op("dve", I("tensor_tensor", out=t[:, 0:ST], in0=cbuf[:, j, :], in1=statsb[:, 0, :], op=ALU.subtract),
                 reads=[r_c, r_stb], writes=[r_t])
            S.op("pool", I("tensor_mul", out=t[:, 0:ST], in0=t[:, 0:ST], in1=statsb[:, 2, :]), reads=[r_t, r_stb], writes=[r_t])
            S.op("act", I("activation", out=mixed[:, j, :], in_=t[:, 0:ST], func=AF.Silu, bias=cc(C, lay, "cn_b", j, j + 1),
                                                    scale=cc(C, lay, "cn_g", j, j + 1)),
                 reads=[r_t, C.r_cst], writes=[r_mixed])
        for gg in range(2 if stage >= 6 else 0):
            P, r_P = get_pair(C)
            for h in range(2):
                g = 2 * gg + h
                for tb in range(NB):
                    S.op("pe", I("matmul", P[:, h, tb * 128:(tb + 1) * 128], lhsT=vnb[:, tb, g * 128:(g + 1) * 128],
                                                                   rhs=wsTb[:, g, :], start=True, stop=True),
                         reads=[r_vn[tb], r_ws], writes=[r_P])
                t, r_t = get_scr(C)
                S.op("dve", I("tensor_tensor", out=t[:, 0:ST], in0=P[:, h, :], in1=cc(C, lay, "sgu_b", g * 512, (g + 1) * 512), op=ALU.add),
                     reads=[r_P, C.r_cst], writes=[r_t])
                S.op("pool", I("tensor_mul", out=mixed[:, 4 + g, :], in0=t[:, 0:ST], in1=ubuf[:, g, :]), reads=[r_t, r_u], writes=[r_mixed])
        if stage >= 7:
            outproj_ln(S, C, lay, mixed, r_mixed, wO_d, xres, r_xres, "b_out", "lnm_g", "lnm_b")
        if stop_after != "mix":
            for tb in range(NB):
                make_T(S, C, xres[:, tb, :], r_xres[tb], xT, r_xT, HALO + tb * 128)
            ffn_block(S, C, lay, xT, r_xT, xres, r_xres, w1_d, w2_d, "lnf_g", "lnf_b")
        ev = S.dma("sp", I("dma_start", out=out_d[st * ST:(st + 1) * ST, :].rearrange("(tb p) d -> p tb d", p=128), in_=xres[:, :, :]),
                   "st_out", reads=r_xres)
        out_evs.append(ev)
    S.wait_all("sp", out_evs[-1:])
    S.emit()
    S.close()
    return nc


def r128(a):
    a = np.asarray(a, np.float32).reshape(1, -1)
    return np.ascontiguousarray(np.broadcast_to(a, (128, a.shape[1])))


def pack_cst(lay, ncst, parts):
    cst = np.zeros((128, ncst), np.float32)
    for k, v in parts.items():
        off, w = lay[k]
        assert v.shape == (128, w), (k, v.shape, w)
        cst[:, off:off + w] = v
    return cst


def host_A(inp, layer=0):
    x = inp["x"]
    i = 0
    w_in = inp["ab_w_in"][i]
    wA = np.ascontiguousarray(w_in[:, :1536].reshape(8, 128, 12, 128).transpose(2, 1, 0, 3)).reshape(12, 128, 1024)
    wV = np.ascontiguousarray(w_in[:, 1536:].reshape(4, 2, 128, 512).transpose(0, 2, 1, 3)).reshape(4, 128, 1024)
    wO = np.ascontiguousarray(inp["ab_w_out"][i].reshape(8, 128, 1024))
    fw = inp["ffn_w_in"][layer]
    w1 = np.ascontiguousarray(fw.reshape(8, 128, 2, NFC, 128).transpose(3, 1, 2, 0, 4)).reshape(NFC, 128, 2048)
    w2 = np.ascontiguousarray(inp["ffn_w_out"][layer].reshape(NFC, 128, 1024))
    b_in = inp["ab_b_in"][i]
    parts = {
        "b_in": np.ascontiguousarray(b_in[:1536].reshape(12, 128).T),
        "conv_w": np.ascontiguousarray(inp["conv_w"][i][:, 0, :].reshape(31, 4, 128).transpose(2, 1, 0)).reshape(128, 124),
        "conv_b": np.ascontiguousarray(inp["conv_b"][i].reshape(4, 128).T),
        "cn_g": np.ascontiguousarray(inp["conv_norm_g"][i].reshape(4, 128).T),
        "cn_b": np.ascontiguousarray(inp["conv_norm_b"][i].reshape(4, 128).T),
        "b_v": r128(b_in[1536:]),
        "sn_g": r128(inp["sgu_norm_g"][i]),
        "sn_b": r128(inp["sgu_norm_b"][i]),
        "sgu_b": r128(np.tile(inp["sgu_b"][i][:, None, :], (1, 4, 1)).reshape(-1)),
        "wsT": np.ascontiguousarray(inp["sgu_w"][i].transpose(2, 0, 1)).reshape(128, 512),
        "b_out": r128(inp["ab_b_out"][i]),
        "lnm_g": r128(inp["ln_mix_g"][layer]), "lnm_b": r128(inp["ln_mix_b"][layer]),
        "lnf_g": r128(inp["ln_ffn_g"][layer]), "lnf_b": r128(inp["ln_ffn_b"][layer]),
    }
    maps = []
    for c in range(8):
        b, hf = c // 2, c % 2
        t0 = hf * NTOK
        xs = np.zeros((HALO + NTOK, 1024), np.float32)
        xs[HALO:] = x[b, t0:t0 + NTOK]
        if hf == 1:
            xs[:HALO] = x[b, t0 - HALO:t0]
        p = dict(parts)
        p["halo"] = np.full((128, 1), float(hf), np.float32)
        maps.append({"xs": xs, "cst": pack_cst(CST_A, NCST_A, p), "wA": wA, "wV": wV, "wO": wO, "w1": w1, "w2": w2})
    return maps


CST_C, NCST_C = _lay([("lnm_g", 1024), ("lnm_b", 1024), ("lnf_g", 1024), ("lnf_b", 1024)])


def build_C():
    nc = bass.Bass("TRN2", target_bir_lowering=False)
    lay = CST_C
    xs = nc.dram_tensor("xs", [NTOK, 1024], F32, kind="ExternalInput").ap()
    ygs = nc.dram_tensor("ygs", [NTOK, 1024], F32, kind="ExternalInput").ap()
    cst_d = nc.dram_tensor("cst", [128, NCST_C], F32, kind="ExternalInput").ap()
    wO_d = nc.dram_tensor("wO", [8, 128, 1024], F32, kind="ExternalInput").ap()
    w1_d = nc.dram_tensor("w1", [NFC, 128, 2048], F32, kind="ExternalInput").ap()
    w2_d = nc.dram_tensor("w2", [NFC, 128, 1024], F32, kind="ExternalInput").ap()
    out_d = nc.dram_tensor("out", [NTOK, 1024], F32, kind="ExternalOutput").ap()
    S = Sched(nc)
    C = Ctx()
    common_setup(S, C, cst_d, NCST_C)
    NB = ST // 128
    xres = S.sbuf("xres", [128, NB, 1024], F32); r_xres = [Res() for _ in range(NB)]
    ygres = S.sbuf("ygres", [128, NB, 1024], F32); r_ygres = [Res() for _ in range(NB)]
    xT = S.sbuf("xT", [128, 8, HALO + ST], BF16); r_xT = Res()
    ygT = S.sbuf("ygT", [128, 8, ST], BF16); r_ygT = Res()
    C.actb = [S.sbuf("actb%d" % i, [128, FG, ST], BF16) for i in range(2)]; C.r_actb = [Res(), Res()]
    out_evs = []
    for st in range(NST):
        r0 = st * ST
        S.dma("sp", I("dma_start", out=xres[:, :, :], in_=xs[r0:r0 + ST, :].rearrange("(tb p) d -> p tb d", p=128)), "ld_x", writes=r_xres)
        S.dma("sp", I("dma_start", out=ygres[:, :, :], in_=ygs[r0:r0 + ST, :].rearrange("(tb p) d -> p tb d", p=128)), "ld_yg", writes=r_ygres)
        for tb in range(NB):
            make_T(S, C, ygres[:, tb, :], r_ygres[tb], ygT, r_ygT, tb * 128)
        outproj_ln(S, C, lay, ygT, r_ygT, wO_d, xres, r_xres, None, "lnm_g", "lnm_b")
        for tb in range(NB):
            make_T(S, C, xres[:, tb, :], r_xres[tb], xT, r_xT, HALO + tb * 128)
        ffn_block(S, C, lay, xT, r_xT, xres, r_xres, w1_d, w2_d, "lnf_g", "lnf_b")
        ev = S.dma("sp", I("dma_start", out=out_d[st * ST:(st + 1) * ST, :].rearrange("(tb p) d -> p tb d", p=128), in_=xres[:, :, :]),
                   "st_out", reads=r_xres)
        out_evs.append(ev)
    S.wait_all("sp", out_evs[-1:])
    S.emit()
    S.close()
    return nc


def host_C(inp, x1, yg, layer=1):
    wO = np.ascontiguousarray(inp["rwkv_w_out"][0].reshape(8, 128, 1024))
    fw = inp["ffn_w_in"][layer]
    w1 = np.ascontiguousarray(fw.reshape(8, 128, 2, NFC, 128).transpose(3, 1, 2, 0, 4)).reshape(NFC, 128, 2048)
    w2 = np.ascontiguousarray(inp["ffn_w_out"][layer].reshape(NFC, 128, 1024))
    cst = pack_cst(CST_C, NCST_C, {
        "lnm_g": r128(inp["ln_mix_g"][layer]), "lnm_b": r128(inp["ln_mix_b"][layer]),
        "lnf_g": r128(inp["ln_ffn_g"][layer]), "lnf_b": r128(inp["ln_ffn_b"][layer])})
    maps = []
    for c in range(8):
        b, hf = c // 2, c % 2
        t0 = hf * NTOK
        maps.append({"xs": np.ascontiguousarray(x1[b, t0:t0 + NTOK]), "ygs": np.ascontiguousarray(yg[b, t0:t0 + NTOK]),
                     "cst": cst, "wO": wO, "w1": w1, "w2": w2})
    return maps


GN_EPS = 64e-5
EM5 = float(np.exp(-0.5))
CST_B, NCST_B = _lay([
    ("mu", 48), ("w0", 512), ("a0", 512), ("k_k", 512), ("k_a", 512), ("ln_g", 512), ("ln_b", 512), ("r_k", 512),
    ("triT", 128), ("blk1", 128), ("sel2", 2), ("pad", 2), ("mask_w", 256), ("mask_a", 128), ("ident2", 64),
])


class _StopB(Exception):
    pass


def build_B(nblk=32, dbg=None):
    nc = bass.Bass("TRN2", target_bir_lowering=False)
    lay = CST_B
    T = nblk * 128
    x_d = nc.dram_tensor("x1f", [4096, 1024], F32, kind="ExternalInput").ap()
    cst_d = nc.dram_tensor("cst", [128, NCST_B], F32, kind="ExternalInput").ap()
    wrkv_d = nc.dram_tensor("wrkv", [3, 128, 8 * 512], F32, kind="ExternalInput").ap()
    lw1_d = nc.dram_tensor("lw1", [128, 8 * 256], F32, kind="ExternalInput").ap()
    w2a2_d = nc.dram_tensor("w2a2", [128, 512], F32, kind="ExternalInput").ap()
    g2_d = nc.dram_tensor("g2", [128, 512], F32, kind="ExternalInput").ap()
    out_d = nc.dram_tensor("yg", [4096, 512], F32, kind="ExternalOutput").ap()
    dbg_d = nc.dram_tensor("dbg", [4096, 512], F32, kind="ExternalOutput").ap() if dbg else None
    S = Sched(nc)
    C = Ctx()
    C.cst = S.sbuf("cst", [128, NCST_B], F32); C.r_cst = Res()
    S.dma("sp", I("dma_start", out=C.cst[:, :], in_=cst_d), "ld_cst", writes=[C.r_cst])
    C.identf = S.sbuf("identf", [128, 128], F32); C.r_identf = Res()
    C.identb = S.sbuf("identb", [128, 128], BF16); C.r_identb = Res()
    S.op("pool", I("memset", C.identf[:, :], 1.0), writes=[C.r_identf])
    S.op("pool", I("affine_select", out=C.identf[:, :], in_=C.identf[:, :], pattern=[[1, 128]], compare_op=ALU.is_equal, fill=0.0,
                   base=0, channel_multiplier=-1), reads=[C.r_identf], writes=[C.r_identf])
    S.op("dve", I("tensor_copy", out=C.identb[:, :], in_=C.identf[:, :]), reads=[C.r_identf], writes=[C.r_identb])
    wrkv = S.sbuf("wrkv", [128, 3, 8 * 512], BF16); r_wrkv = Res()
    for n in range(3):
        S.dma("pool", I("dma_start", out=wrkv[:, n, :], in_=wrkv_d[n]), "ld_w%d" % n, writes=[r_wrkv])
    lw1 = S.sbuf("lw1", [128, 8, 256], BF16); r_lw1 = Res()
    lw1f = S.sbuf("lw1f", [128, 8, 256], F32); r_lw1f = Res()
    lw1s = S.sbuf("lw1s", [128, 8, 256], BF16); r_lw1s = Res()
    S.dma("sp", I("dma_start", out=lw1f[:, :, :], in_=lw1_d.rearrange("p (k c) -> p k c", k=8)), "ld_lw1", writes=[r_lw1f])
    S.op("dve", I("tensor_copy", out=lw1[:, :, :], in_=lw1f[:, :, :]), reads=[r_lw1f], writes=[r_lw1])
    mu = lambda i, kc: cc(C, lay, "mu", i * 8 + kc, i * 8 + kc + 1)
    for kc in range(8):
        for (i, c0, c1) in ((1, 0, 64), (4, 64, 128), (5, 128, 256)):
            S.op("dve", I("tensor_scalar_mul", out=lw1s[:, kc, c0:c1], in0=lw1f[:, kc, c0:c1], scalar1=mu(i, kc)),
                 reads=[r_lw1f, C.r_cst], writes=[r_lw1s])
    w2a2 = S.sbuf("w2a2", [128, 512], BF16); r_w2a2 = Res()
    S.dma("pool", I("dma_start", out=w2a2[:, :], in_=w2a2_d), "ld_w2a2", writes=[r_w2a2])
    g2 = S.sbuf("g2", [128, 512], BF16); r_g2 = Res()
    S.dma("pool", I("dma_start", out=g2[:, :], in_=g2_d), "ld_g2", writes=[r_g2])

    pm = S.psum("pm", [128, 6, 512], F32); r_pm = [Res() for _ in range(6)]
    pt = S.psum("pt", [128, 2, 1024], BF16); r_pt = [Res(), Res()]
    st = {"pm": 0, "pt": 0}

    def bank():
        i = st["pm"] % 6
        st["pm"] += 1
        return pm[:, i, :], r_pm[i]

    def tbank():
        i = st["pt"] % 2
        st["pt"] += 1
        return pt[:, i, :], r_pt[i]

    def sb(name, shape, dt=F32):
        return S.sbuf(name, shape, dt), Res()

    xres, r_x = sb("xres", [128, 1024])
    xb, r_xb = sb("xb", [128, 1024], BF16)
    xT, r_xT = sb("xT", [128, 8, 129], BF16)
    xxT, r_xxT = sb("xxT", [128, 8, 128], BF16)
    xmT = [sb("xmT%d" % i, [128, 8, 128], BF16) for i in range(3)]
    rkv = [sb("rkv%d" % i, [128, 512]) for i in range(3)]
    l1, r_l1 = sb("l1", [128, 128], BF16)
    g1, r_g1 = sb("g1", [128, 128], BF16)
    sw, r_sw = sb("sw", [128, 512])
    av, r_av = sb("av", [128, 512])
    gt, r_gt = sb("gt", [128, 512])
    Lsb, r_L = sb("Lsb", [128, 512])
    ex = [sb("ex%d" % i, [128, 512]) for i in range(4)]
    t1, r_t1 = sb("t1", [128, 512])
    t2, r_t2 = sb("t2", [128, 512])
    kk, r_kk = sb("kk", [128, 512])
    kp, r_kp = sb("kp", [128, 512])
    bv, r_bv = sb("bv", [128, 512])
    sm8, r_sm8 = sb("sm8", [128, 64])
    rtl, r_rtl = sb("rtl", [128, 512], BF16)
    ktl, r_ktl = sb("ktl", [128, 512], BF16)
    btl, r_btl = sb("btl", [128, 512], BF16)
    atl, r_atl = sb("atl", [128, 512], BF16)
    khz = [sb("khz%d" % i, [128, 512], BF16) for i in range(2)]
    bhz = [sb("bhz%d" % i, [128, 512], BF16) for i in range(2)]
    vb, r_vb = sb("vb", [128, 512], BF16)
    AR2, r_AR2 = sb("AR2", [128, 4, 256], BF16)
    BTf, r_BTf = sb("BTf", [128, 4, 128], BF16)
    RTf, r_RTf = sb("RTf", [128, 4, 128])
    ATz = [sb("ATz%d" % i, [128, 4, 128], BF16) for i in range(2)]
    KTz = [sb("KTz%d" % i, [128, 4, 128], BF16) for i in range(2)]
    BTz = [sb("BTz%d" % i, [128, 4, 128], BF16) for i in range(2)]
    Wb, r_Wb = sb("Wb", [128, 8, 256], BF16)
    Wk, r_Wk = sb("Wk", [128, 8, 256], BF16)
    Nc = [sb("N%d" % i, [128, 8, 128], BF16) for i in range(2)]
    Ac = [sb("A%d" % i, [128, 8, 128], BF16) for i in range(2)]
    Xc = [sb("X%d" % i, [128, 8, 128], BF16) for i in range(2)]
    gC, r_gC = sb("gC", [128, 4, 2])
    dG, r_dG = sb("dG", [128, 8, 64])
    MTb, r_MTb = sb("MTb", [128, 8, 128])
    Gs, r_Gs = sb("Gs", [128, 4, 2, 64])
    RpTp = [[sb("RpTp%d_%d" % (i, j), [128, 4, 128]) for j in range(2)] for i in range(2)]
    Y0, r_Y0 = sb("Y0", [128, 512])
    for t_, r_t_ in khz + bhz + ATz + KTz + BTz + [(MTb, r_MTb)] + RpTp[0] + RpTp[1]:
        S.op("pool", I("memset", t_[tuple(slice(None) for _ in t_.shape)], 0.0), writes=[r_t_])
    H, r_H = sb("H", [128, 4, 64])
    y, r_y = sb("y", [128, 512])
    yo, r_yo = sb("yo", [128, 512])
    S.op("pool", I("memset", H[:, :, :], 0.0), writes=[r_H])
    S.op("pool", I("memset", xT[:, :, 0:1], 0.0), writes=[r_xT])

    TT = lambda eng, out, a, b, op, rd, wr: S.op(eng, I("tensor_tensor", out=out, in0=a, in1=b, op=op), reads=rd, writes=wr)
    cs = lambda name, a=0, b=None: cc(C, lay, name, a, b)
    h8 = lambda ap: ap.rearrange("p (h j) -> p h j", h=8)
    out_evs = []
    BST = int(os.environ.get('B_STAGE', '99'))
    for blk in range(nblk):
      r0 = blk * 128
      try:
        S.dma("sp", I("dma_start", out=xres[:, :], in_=x_d[r0:r0 + 128, :]), "ld_x", writes=[r_x])
        if blk > 0:
            S.op("dve", I("tensor_copy", out=xT[:, :, 0:1], in_=xT[:, :, 128:129]), reads=[r_xT], writes=[r_xT])
        S.op("act", I("copy", out=xb[:, :], in_=xres[:, :]), reads=[r_x], writes=[r_xb])
        p_, r_p = tbank()
        for kc in range(8):
            S.op("pe", I("transpose", p_[:, kc * 128:(kc + 1) * 128], xb[:, kc * 128:(kc + 1) * 128], C.identb[:, :]), reads=[r_xb, C.r_identb], writes=[r_p])
        S.op("dve", I("tensor_copy", out=xT[:, :, 1:129], in_=p_.rearrange("p (k c) -> p k c", k=8)), reads=[r_p], writes=[r_xT])
        TT("pool", xxT[:, :, :], xT[:, :, 0:128], xT[:, :, 1:129], ALU.subtract, [r_xT], [r_xxT])
        for n, mi in enumerate((0, 2, 3)):
            for kc in range(8):
                S.op("dve", I("scalar_tensor_tensor", out=xmT[n][0][:, kc, :], in0=xxT[:, kc, :], scalar=mu(mi, kc), in1=xT[:, kc, 1:129],
                              op0=ALU.mult, op1=ALU.add), reads=[r_xxT, r_xT, C.r_cst], writes=[xmT[n][1]])
        if BST <= 1:
            raise _StopB()
        for n in range(3):
            P, r_P = bank()
            for kc in range(8):
                S.op("pe", I("matmul", P, lhsT=xmT[n][0][:, kc, :], rhs=wrkv[:, n, kc * 512:(kc + 1) * 512], start=(kc == 0), stop=(kc == 7)),
                     reads=[xmT[n][1], r_wrkv], writes=[r_P])
            S.op("act", I("copy", out=rkv[n][0][:, :], in_=P), reads=[r_P], writes=[rkv[n][1]])
        r_, r_r = rkv[0]; k_, r_k = rkv[1]; v_, r_v = rkv[2]
        if BST <= 2:
            raise _StopB()
        for which, (c0, dst, r_dst) in enumerate(((0, l1, r_l1), (128, g1, r_g1))):
            P, r_P = bank()
            for kc in range(8):
                S.op("pe", I("matmul", P[:, 0:128], lhsT=lw1[:, kc, c0:c0 + 128], rhs=xT[:, kc, 1:129], start=(kc == 0), stop=False),
                     reads=[r_lw1, r_xT], writes=[r_P])
            for kc in range(8):
                S.op("pe", I("matmul", P[:, 0:128], lhsT=lw1s[:, kc, c0:c0 + 128], rhs=xxT[:, kc, :], start=False, stop=(kc == 7)),
                     reads=[r_lw1s, r_xxT], writes=[r_P])
            if which == 0:
                S.op("act", I("activation", out=l1[0:64, :], in_=P[0:64, 0:128], func=AF.Tanh), reads=[r_P], writes=[r_l1])
                S.op("act", I("copy", out=l1[64:128, :], in_=P[64:128, 0:128]), reads=[r_P], writes=[r_l1])
            else:
                S.op("act", I("activation", out=g1[:, :], in_=P[:, 0:128], func=AF.Sigmoid), reads=[r_P], writes=[r_g1])
        P, r_P = bank()
        S.op("pe", I("matmul", P, lhsT=l1[0:64, :], rhs=w2a2[0:64, :], start=True, stop=True), reads=[r_l1, r_w2a2], writes=[r_P])
        TT("dve", t1[:, :], P, cs("w0"), ALU.add, [r_P, C.r_cst], [r_t1])
        S.op("act", I("activation", out=sw[:, :], in_=t1[:, :], func=AF.Sigmoid), reads=[r_t1], writes=[r_sw])
        P, r_P = bank()
        S.op("pe", I("matmul", P, lhsT=l1[64:128, :], rhs=w2a2[64:128, :], start=True, stop=True), reads=[r_l1, r_w2a2], writes=[r_P])
        TT("dve", t2[:, :], P, cs("a0"), ALU.add, [r_P, C.r_cst], [r_t2])
        S.op("act", I("activation", out=av[:, :], in_=t2[:, :], func=AF.Sigmoid), reads=[r_t2], writes=[r_av])
        P, r_P = bank()
        S.op("pe", I("matmul", P, lhsT=g1[:, :], rhs=g2[:, :], start=True, stop=True), reads=[r_g1, r_g2], writes=[r_P])
        S.op("act", I("copy", out=gt[:, :], in_=P), reads=[r_P], writes=[r_gt])
        if BST <= 3:
            raise _StopB()
        PL, r_PL = bank()
        S.op("pe", I("matmul", PL, lhsT=cs("triT"), rhs=sw[:, :], start=True, stop=True), reads=[C.r_cst, r_sw], writes=[r_PL])
        PC, r_PC = bank()
        S.op("pe", I("matmul", PC, lhsT=cs("blk1"), rhs=sw[:, :], start=True, stop=True), reads=[C.r_cst, r_sw], writes=[r_PC])
        Pg, r_Pg = bank()
        for m in range(4):
            S.op("pe", I("matmul", Pg[:, 2 * m:2 * m + 2], lhsT=sw[:, m * 128:(m + 1) * 128], rhs=cs("sel2"), start=True, stop=True),
                 reads=[C.r_cst, r_sw], writes=[r_Pg])
        S.op("act", I("activation", out=gC[:, :, :], in_=Pg[:, 0:8].rearrange("p (m c) -> p m c", m=4), func=AF.Exp), reads=[r_Pg], writes=[r_gC])
        S.op("act", I("copy", out=Lsb[:, :], in_=PL), reads=[r_PL], writes=[r_L])
        eL, r_eL = ex[0]; enL, r_enL = ex[1]; eCL, r_eCL = ex[2]; eLm, r_eLm = ex[3]
        S.op("act", I("activation", out=eL[:, :], in_=PL, func=AF.Exp), reads=[r_PL], writes=[r_eL])
        S.op("act", I("activation", out=enL[:, :], in_=PL, func=AF.Exp, scale=-1.0), reads=[r_PL], writes=[r_enL])
        TT("dve", eCL[:, :], PC, Lsb[:, :], ALU.subtract, [r_PC, r_L], [r_eCL])
        S.op("act", I("activation", out=eCL[:, :], in_=eCL[:, :], func=AF.Exp), reads=[r_eCL], writes=[r_eCL])
        S.op("dve", I("scalar_tensor_tensor", out=eLm[:, :], in0=sw[:, :], scalar=EM5, in1=Lsb[:, :], op0=ALU.mult, op1=ALU.add),
             reads=[r_sw, r_L], writes=[r_eLm])
        S.op("act", I("activation", out=eLm[:, :], in_=eLm[:, :], func=AF.Exp), reads=[r_eLm], writes=[r_eLm])
        if BST <= 4:
            raise _StopB()
        TT("pool", t1[:, :], k_[:, :], cs("k_k"), ALU.mult, [r_k, C.r_cst], [r_t1])
        TT("pool", t2[:, :], t1[:, :], t1[:, :], ALU.mult, [r_t1], [r_t2])
        S.op("dve", I("tensor_reduce", out=sm8[:, 0:8], in_=h8(t2[:, :]), axis=AX.X, op=ALU.add), reads=[r_t2], writes=[r_sm8])
        S.op("act", I("sqrt", out=sm8[:, 8:16], in_=sm8[:, 0:8]), reads=[r_sm8], writes=[r_sm8])
        S.op("dve", I("tensor_scalar_max", out=sm8[:, 8:16], in0=sm8[:, 8:16], scalar1=1e-12), reads=[r_sm8], writes=[r_sm8])
        S.op("dve", I("reciprocal", out=sm8[:, 16:24], in_=sm8[:, 8:16]), reads=[r_sm8], writes=[r_sm8])
        TT("dve", h8(kk[:, :]), h8(t1[:, :]), sm8[:, 16:24].unsqueeze(2).to_broadcast([128, 8, 64]), ALU.mult, [r_t1, r_sm8], [r_kk])
        S.op("dve", I("scalar_tensor_tensor", out=t2[:, :], in0=av[:, :], scalar=-1.0, in1=cs("k_a"), op0=ALU.add, op1=ALU.mult),
             reads=[r_av, C.r_cst], writes=[r_t2])
        S.op("dve", I("scalar_tensor_tensor", out=kp[:, :], in0=t2[:, :], scalar=1.0, in1=k_[:, :], op0=ALU.add, op1=ALU.mult),
             reads=[r_t2, r_k], writes=[r_kp])
        TT("pool", bv[:, :], kk[:, :], av[:, :], ALU.mult, [r_kk, r_av], [r_bv])
        TT("pool", t1[:, :], r_[:, :], kp[:, :], ALU.mult, [r_r, r_kp], [r_t1])
        TT("pool", t1[:, :], t1[:, :], cs("r_k"), ALU.mult, [r_t1, C.r_cst], [r_t1])
        S.op("dve", I("tensor_reduce", out=sm8[:, 24:32], in_=h8(t1[:, :]), axis=AX.X, op=ALU.add), reads=[r_t1], writes=[r_sm8])
        TT("dve", rtl[:, :], r_[:, :], eL[:, :], ALU.mult, [r_r, r_eL], [r_rtl])
        TT("pool", ktl[:, :], kp[:, :], enL[:, :], ALU.mult, [r_kp, r_enL], [r_ktl])
        TT("pool", btl[:, :], bv[:, :], enL[:, :], ALU.mult, [r_bv, r_enL], [r_btl])
        S.op("dve", I("scalar_tensor_tensor", out=atl[:, :], in0=kk[:, :], scalar=-1.0, in1=eLm[:, :], op0=ALU.mult, op1=ALU.mult),
             reads=[r_kk, r_eLm], writes=[r_atl])
        S.op("act", I("copy", out=vb[:, :], in_=v_[:, :]), reads=[r_v], writes=[r_vb])
        for cp in range(2):
            pc = slice(cp * 64, cp * 64 + 64)
            TT("dve", khz[cp][0][pc, :], kp[pc, :], eCL[pc, :], ALU.mult, [r_kp, r_eCL], [khz[cp][1]])
            TT("pool", bhz[cp][0][pc, :], bv[pc, :], eCL[pc, :], ALU.mult, [r_bv, r_eCL], [bhz[cp][1]])
        S.op("pool", I("tensor_copy", out=Xc[0][0][:, :, 0:64], in_=h8(atl[:, :])), reads=[r_atl], writes=[Xc[0][1]])
        if BST <= 5:
            raise _StopB()
        for src, r_src, kind in ((atl, r_atl, 0), (rtl, r_rtl, 1), (ktl, r_ktl, 2), (btl, r_btl, 3)):
            p_, r_p = tbank()
            for m in range(4):
                S.op("pe", I("transpose", p_[:, m * 128:(m + 1) * 128], src[:, m * 128:(m + 1) * 128], C.identb[:, :]), reads=[r_src, C.r_identb], writes=[r_p])
            pv = p_[:, 0:512].rearrange("p (m t) -> p m t", m=4)
            if kind == 0:
                S.op("dve", I("tensor_copy", out=AR2[:, :, 0:128], in_=pv), reads=[r_p], writes=[r_AR2])
                zt = ATz
            elif kind == 1:
                S.op("dve", I("tensor_copy", out=AR2[:, :, 128:256], in_=pv), reads=[r_p], writes=[r_AR2])
                S.op("act", I("copy", out=RTf[:, :, :], in_=pv), reads=[r_p], writes=[r_RTf])
                zt = None
            elif kind == 2:
                zt = KTz
            else:
                S.op("dve", I("tensor_copy", out=BTf[:, :, :], in_=pv), reads=[r_p], writes=[r_BTf])
                zt = BTz
            if zt is not None:
                S.op("act", I("copy", out=zt[0][0][0:64, :, :], in_=pv[0:64]), reads=[r_p], writes=[zt[0][1]])
                S.op("act", I("copy", out=zt[1][0][64:128, :, :], in_=pv[64:128]), reads=[r_p], writes=[zt[1][1]])
        if BST <= 6:
            raise _StopB()
        mw = cs("mask_w").unsqueeze(1).to_broadcast([128, 2, 256])
        ma = cs("mask_a").unsqueeze(1).to_broadcast([128, 2, 128])
        for m in range(4):
            Pb, r_Pb = bank(); Pk, r_Pk = bank(); Pa, r_Pa = bank()
            for hp in range(2):
                S.op("pe", I("matmul", Pb[:, hp * 256:(hp + 1) * 256], lhsT=BTz[hp][0][:, m, :], rhs=AR2[:, m, :], start=True, stop=True),
                     reads=[BTz[hp][1], r_AR2], writes=[r_Pb])
                S.op("pe", I("matmul", Pk[:, hp * 256:(hp + 1) * 256], lhsT=KTz[hp][0][:, m, :], rhs=AR2[:, m, :], start=True, stop=True),
                     reads=[KTz[hp][1], r_AR2], writes=[r_Pk])
                S.op("pe", I("matmul", Pa[:, hp * 128:(hp + 1) * 128], lhsT=ATz[hp][0][:, m, :], rhs=BTf[:, m, :], start=True, stop=True),
                     reads=[ATz[hp][1], r_BTf], writes=[r_Pa])
            TT("dve", Wb[:, 2 * m:2 * m + 2, :], Pb.rearrange("p (h c) -> p h c", h=2), mw, ALU.mult, [r_Pb, C.r_cst], [r_Wb])
            TT("dve", Wk[:, 2 * m:2 * m + 2, :], Pk.rearrange("p (h c) -> p h c", h=2), mw, ALU.mult, [r_Pk, C.r_cst], [r_Wk])
            TT("dve", Ac[0][0][:, 2 * m:2 * m + 2, :], Pa[:, 0:256].rearrange("p (h c) -> p h c", h=2), ma, ALU.mult, [r_Pa, C.r_cst], [Ac[0][1]])
        S.op("act", I("copy", out=Nc[0][0][:, :, :], in_=Wb[:, :, 0:128]), reads=[r_Wb], writes=[Nc[0][1]])
        PX, r_PX = bank()
        for h in range(8):
            S.op("pe", I("matmul", PX[:, h * 64:(h + 1) * 64], lhsT=Wk[:, h, 0:128], rhs=vb[:, h * 64:(h + 1) * 64], start=True, stop=True),
                 reads=[r_Wk, r_vb], writes=[r_PX])
        S.op("act", I("copy", out=Xc[0][0][:, :, 64:128], in_=h8(PX)), reads=[r_PX], writes=[Xc[0][1]])
        if BST <= 7:
            raise _StopB()
        cur = 0
        for lvl in range(6):
            Nt, r_N = Nc[cur]; At, r_A = Ac[cur]; Xt, r_X = Xc[cur]
            Nn, r_Nn = Nc[1 - cur]; An, r_An = Ac[1 - cur]; Xn, r_Xn = Xc[1 - cur]
            for half in range(2):
                PX, r_PX = bank()
                for hl in range(4):
                    h = half * 4 + hl
                    S.op("pe", I("matmul", PX[:, hl * 128:(hl + 1) * 128], lhsT=Nt[:, h, :], rhs=Xt[:, h, :], start=True, stop=True),
                         reads=[r_N, r_X], writes=[r_PX])
                TT("dve", Xn[:, half * 4:half * 4 + 4, :], PX.rearrange("p (h c) -> p h c", h=4), Xt[:, half * 4:half * 4 + 4, :], ALU.add, [r_PX, r_X], [r_Xn])
            if lvl < 5:
                for half in range(2):
                    PN, r_PN = bank()
                    for hl in range(4):
                        h = half * 4 + hl
                        S.op("pe", I("matmul", PN[:, hl * 128:(hl + 1) * 128], lhsT=At[:, h, :], rhs=Nt[:, h, :], start=True, stop=True),
                             reads=[r_N, r_A], writes=[r_PN])
                    S.op("act", I("copy", out=Nn[:, half * 4:half * 4 + 4, :], in_=PN.rearrange("p (h c) -> p h c", h=4)), reads=[r_PN], writes=[r_Nn])
            if lvl < 4:
                for half in range(2):
                    PA, r_PA = bank()
                    for hl in range(4):
                        h = half * 4 + hl
                        S.op("pe", I("matmul", PA[:, hl * 128:(hl + 1) * 128], lhsT=Nt[:, h, :], rhs=At[:, h, :], start=True, stop=True),
                             reads=[r_N, r_A], writes=[r_PA])
                    S.op("pool" if False else "act", I("copy", out=An[:, half * 4:half * 4 + 4, :], in_=PA.rearrange("p (h c) -> p h c", h=4)), reads=[r_PA], writes=[r_An])
            cur = 1 - cur
        X6, r_X6 = Xc[0]
        if BST <= 8:
            raise _StopB()
        PY, r_PY = bank()
        for h in range(8):
            S.op("pe", I("matmul", PY[:, h * 64:(h + 1) * 64], lhsT=Wb[:, h, 128:256], rhs=X6[:, h, 64:128], start=True, stop=False),
                 reads=[r_Wb, r_X6], writes=[r_PY])
            S.op("pe", I("matmul", PY[:, h * 64:(h + 1) * 64], lhsT=Wk[:, h, 128:256], rhs=vb[:, h * 64:(h + 1) * 64], start=False, stop=True),
                 reads=[r_Wk, r_vb], writes=[r_PY])
        S.op("act", I("copy", out=Y0[:, :], in_=PY), reads=[r_PY], writes=[r_Y0])
        PM, r_PM = bank(); PG, r_PG = bank(); PR, r_PR = bank()
        for h in range(8):
            m, hp = h // 2, h % 2
            ph = slice(hp * 64, hp * 64 + 64)
            hs = slice(h * 64, (h + 1) * 64)
            for cp in range(2):
                o = (m * 2 + cp) * 64
                S.op("pe", I("matmul", PM[ph, o:o + 64], lhsT=X6[:, h, 0:64], rhs=bhz[cp][0][:, hs], start=True, stop=True),
                     reads=[r_X6, bhz[cp][1]], writes=[r_PM])
                S.op("pe", I("matmul", PG[ph, o:o + 64], lhsT=bhz[cp][0][:, hs], rhs=X6[:, h, 64:128], start=True, stop=False),
                     reads=[r_X6, bhz[cp][1]], writes=[r_PG])
                S.op("pe", I("matmul", PG[ph, o:o + 64], lhsT=khz[cp][0][:, hs], rhs=vb[:, hs], start=False, stop=True),
                     reads=[khz[cp][1], r_vb], writes=[r_PG])
                S.op("pe", I("matmul", PR[ph, o:o + 64], lhsT=X6[:, h, 0:64], rhs=Wb[:, h, 128 + cp * 64:128 + (cp + 1) * 64], start=True, stop=True),
                     reads=[r_X6, r_Wb], writes=[r_PR])
        TT("pool", dG[:, :, :], cs("ident2").unsqueeze(1).to_broadcast([128, 8, 64]),
           gC[:, :, :].rearrange("p m c -> p (m c)").unsqueeze(2).to_broadcast([128, 8, 64]), ALU.mult, [C.r_cst, r_gC], [r_dG])
        S.op("act", I("copy", out=Gs[:, :, :, :].rearrange("p m c j -> p (m c j)"), in_=PG), reads=[r_PG], writes=[r_Gs])
        PMv = PM.rearrange("p (a j) -> p a j", j=64)
        PRv = PR.rearrange("p (m c t) -> p m c t", m=4, c=2)
        RTv = RTf[:, :, :].rearrange("p m (c t) -> p m c t", c=2)
        for hp in range(2):
            ph = slice(hp * 64, hp * 64 + 64)
            TT("dve", MTb[ph, :, hp * 64:(hp + 1) * 64], PMv[ph], dG[ph, :, :], ALU.add, [r_PM, r_dG], [r_MTb])
            for cp in range(2):
                TT("dve", RpTp[hp][cp][0][ph, :, cp * 64:(cp + 1) * 64], PRv[ph, :, cp, :], RTv[ph, :, cp, :], ALU.add, [r_PR, r_RTf], [RpTp[hp][cp][1]])
        if BST <= 9:
            raise _StopB()
        for cp in range(2):
            pc = slice(cp * 64, cp * 64 + 64)
            PYc, r_PYc = bank(); PH, r_PH = bank()
            for h in range(8):
                m, hp = h // 2, h % 2
                S.op("pe", I("matmul", PYc[:, h * 64:(h + 1) * 64], lhsT=RpTp[hp][cp][0][:, m, :], rhs=H[:, m, :], start=True, stop=True),
                     reads=[RpTp[hp][cp][1], r_H], writes=[r_PYc])
            for m in range(4):
                S.op("pe", I("matmul", PH[:, m * 64:(m + 1) * 64], lhsT=MTb[:, m * 2 + cp, :], rhs=H[:, m, :], start=True, stop=True),
                     reads=[r_MTb, r_H], writes=[r_PH])
            TT("dve", y[pc, :], PYc[pc, :], Y0[pc, :], ALU.add, [r_PYc, r_Y0], [r_y])
            TT("dve", H[:, :, :], PH[:, 0:256].rearrange("p (m i) -> p m i", m=4), Gs[:, :, cp, :], ALU.add, [r_PH, r_Gs], [r_H])
        if BST <= 10:
            raise _StopB()
        S.op("dve", I("tensor_reduce", out=sm8[:, 32:40], in_=h8(y[:, :]), axis=AX.X, op=ALU.add), reads=[r_y], writes=[r_sm8])
        TT("pool", t1[:, :], y[:, :], y[:, :], ALU.mult, [r_y], [r_t1])
        S.op("dve", I("tensor_reduce", out=sm8[:, 40:48], in_=h8(t1[:, :]), axis=AX.X, op=ALU.add), reads=[r_t1], writes=[r_sm8])
        S.op("dve", I("tensor_scalar_mul", out=sm8[:, 32:40], in0=sm8[:, 32:40], scalar1=1.0 / 64), reads=[r_sm8], writes=[r_sm8])
        TT("dve", sm8[:, 48:56], sm8[:, 32:40], sm8[:, 32:40], ALU.mult, [r_sm8], [r_sm8])
        S.op("dve", I("scalar_tensor_tensor", out=sm8[:, 40:48], in0=sm8[:, 40:48], scalar=1.0 / 64, in1=sm8[:, 48:56], op0=ALU.mult, op1=ALU.subtract),
             reads=[r_sm8], writes=[r_sm8])
        S.op("dve", I("tensor_scalar_add", out=sm8[:, 40:48], in0=sm8[:, 40:48], scalar1=GN_EPS), reads=[r_sm8], writes=[r_sm8])
        S.op("act", I("sqrt", out=sm8[:, 40:48], in_=sm8[:, 40:48]), reads=[r_sm8], writes=[r_sm8])
        S.op("dve", I("reciprocal", out=sm8[:, 48:56], in_=sm8[:, 40:48]), reads=[r_sm8], writes=[r_sm8])
        bc8 = lambda a, b: sm8[:, a:b].unsqueeze(2).to_broadcast([128, 8, 64])
        TT("dve", h8(t1[:, :]), h8(y[:, :]), bc8(32, 40), ALU.subtract, [r_y, r_sm8], [r_t1])
        TT("dve", h8(t1[:, :]), h8(t1[:, :]), bc8(48, 56), ALU.mult, [r_t1, r_sm8], [r_t1])
        TT("pool", t1[:, :], t1[:, :], cs("ln_g"), ALU.mult, [r_t1, C.r_cst], [r_t1])
        TT("pool", t1[:, :], t1[:, :], cs("ln_b"), ALU.add, [r_t1, C.r_cst], [r_t1])
        TT("dve", h8(t2[:, :]), h8(v_[:, :]), bc8(24, 32), ALU.mult, [r_v, r_sm8], [r_t2])
        TT("pool", t1[:, :], t1[:, :], t2[:, :], ALU.add, [r_t1, r_t2], [r_t1])
        TT("pool", yo[:, :], t1[:, :], gt[:, :], ALU.mult, [r_t1, r_gt], [r_yo])
      except _StopB:
        pass
      if True:
        ev = S.dma("sp", I("dma_start", out=out_d[r0:r0 + 128, :], in_=yo[:, :]), "st_out", reads=[r_yo])
        out_evs.append(ev)
        if dbg:
            src, r_src = {"r": (r_, r_r), "k": (kp, r_kp), "kk": (kk, r_kk), "a": (av, r_av), "v": (v_, r_v), "y": (y, r_y), "eL": (eL, r_eL), "g": (gt, r_gt)}[dbg]
            ev = S.dma("sp", I("dma_start", out=dbg_d[r0:r0 + 128, :], in_=src[:, :]), "st_dbg", reads=[r_src])
            out_evs.append(ev)
    S.wait_all("sp", out_evs[-2:])
    S.emit()
    S.close()
    return nc


def host_B(inp, x1):
    i = 0
    tri = np.zeros((128, 128), np.float32)
    blk = np.zeros((128, 128), np.float32)
    sel = np.zeros((128, 2), np.float32)
    mask_w = np.zeros((128, 256), np.float32)
    mask_a = np.zeros((128, 128), np.float32)
    for p in range(128):
        c, s_ = p // 64, p % 64
        tri[p, c * 64 + s_:c * 64 + 64] = -EM5
        blk[p, c * 64:c * 64 + 64] = -EM5
        sel[p, c] = -EM5
        mask_w[p, c * 64 + s_ + 1:c * 64 + 64] = 1.0
        mask_w[p, 128 + c * 64 + s_:128 + c * 64 + 64] = 1.0
        mask_a[p, c * 64:c * 64 + s_] = 1.0
    ident2 = np.concatenate([np.eye(64, dtype=np.float32)] * 2, 0)
    maps = []
    for c in range(8):
        b, hh = c // 2, c % 2
        ch = slice(hh * 512, hh * 512 + 512)
        wrkv = np.ascontiguousarray(inp["rwkv_w_rkv"][i][:, :, ch].reshape(3, 8, 128, 512).transpose(0, 2, 1, 3)).reshape(3, 128, 4096)
        lw1 = np.concatenate([inp["rwkv_w_w1"][i], inp["rwkv_a_w1"][i], inp["rwkv_g_w1"][i]], 1)
        lw1 = np.ascontiguousarray(lw1.reshape(8, 128, 256).transpose(1, 0, 2)).reshape(128, 2048)
        w2a2 = np.ascontiguousarray(np.concatenate([inp["rwkv_w_w2"][i][:, ch], inp["rwkv_a_w2"][i][:, ch]], 0))
        g2 = np.ascontiguousarray(inp["rwkv_g_w2"][i][:, ch])
        parts = {
            "mu": np.ascontiguousarray(inp["rwkv_mu"][i].reshape(6, 8, 128).transpose(2, 0, 1)).reshape(128, 48),
            "w0": r128(inp["rwkv_w0"][i][ch]), "a0": r128(inp["rwkv_a0"][i][ch]),
            "k_k": r128(inp["rwkv_k_k"][i][ch]), "k_a": r128(inp["rwkv_k_a"][i][ch]),
            "ln_g": r128(inp["rwkv_ln_g"][i][ch]), "ln_b": r128(inp["rwkv_ln_b"][i][ch]),
            "r_k": r128(inp["rwkv_r_k"][i].reshape(-1)[ch]),
            "triT": tri, "blk1": blk, "sel2": sel, "mask_w": mask_w, "mask_a": mask_a, "ident2": ident2,
        }
        maps.append({"x1f": np.ascontiguousarray(x1[b]), "cst": pack_cst(CST_B, NCST_B, parts), "wrkv": wrkv, "lw1": lw1, "w2a2": w2a2, "g2": g2})
    return maps


from concourse.bass_utils import run_bass_kernel_spmd


def _pairs_rows(results, key):
    return np.stack([np.concatenate([results[2 * b][key], results[2 * b + 1][key]], 0) for b in range(4)])


def kernel(**inputs):
    inp = {k: np.asarray(v) for k, v in inputs.items()}
    cores = list(range(8))
    resA = run_bass_kernel_spmd(build_A(), host_A(inp, 0), core_ids=cores)
    x1 = _pairs_rows(resA.results, "out")
    resB = run_bass_kernel_spmd(build_B(), host_B(inp, x1), core_ids=cores)
    yg = np.stack([np.concatenate([resB.results[2 * b]["yg"], resB.results[2 * b + 1]["yg"]], 1) for b in range(4)])
    resC = run_bass_kernel_spmd(build_C(), host_C(inp, x1, yg), core_ids=cores)
    out = _pairs_rows(resC.results, "out")
    return np.ascontiguousarray(out.astype(np.float32))
```
